# Optimizing a Trainium2 kernel written in Bass

```python
import math
import jax, jax.numpy as jnp
from jax import lax
import numpy as np

D_MODEL = 1024
BATCH = 16
SEQ = 2048
DEPTH = 1

SSM_WIDTH = D_MODEL // 2
SSM_GROUP = 16
SSM_GROUPS = SSM_WIDTH // SSM_GROUP
SSM_STATE = 64
N_DIR = 2
DT_MIN = 1e-3
DT_MAX = 1e-1
FFT_WIDTH = D_MODEL // 2
FFT_GROUPS = 4
FFT_GROUP = FFT_WIDTH // FFT_GROUPS
D_FF = 4 * D_MODEL
D_PLE = 256
EPS = 1e-6
IN_WIDTH = SSM_WIDTH + FFT_WIDTH + 2 * D_MODEL

kernel_name = 'hybrid_s5_fnet_gated_encoder_block'


def rms_norm(x, g):
    x32 = x.astype(jnp.float32)
    y = x32 * lax.rsqrt(jnp.mean(x32 * x32, axis=-1, keepdims=True) + EPS)
    return (y * g.astype(jnp.float32)).astype(x.dtype)


def _ssm_combine(left, right):
    a1, b1 = left
    a2, b2 = right
    return (a1 * a2, a2 * b1 + b2)


def s5_scan(u32, lam_re, lam_im, log_dt, b_re, b_im, c_re, c_im, reverse):
    f32 = jnp.float32
    lam = lax.complex(lam_re.astype(f32), lam_im.astype(f32))
    dt = jnp.exp(log_dt.astype(f32))[:, None]
    a_bar = jnp.exp(lam * dt)
    b = lax.complex(b_re.astype(f32), b_im.astype(f32))
    b_bar = ((a_bar - 1.0) / lam)[:, :, None] * b
    bu = jnp.einsum('bsgh,gph->bsgp', u32.astype(jnp.complex64), b_bar)
    a_seq = jnp.broadcast_to(a_bar, (1, u32.shape[1]) + a_bar.shape)
    _, states = lax.associative_scan(_ssm_combine, (a_seq, bu), reverse=reverse, axis=1)
    c = lax.complex(c_re.astype(f32), c_im.astype(f32))
    return jnp.real(jnp.einsum('bsgp,ghp->bsgh', states, c))


def s5_branch(u, lam_re, lam_im, log_dt, b_re, b_im, c_re, c_im, d_skip, w_glu):
    bsz, seq = u.shape[0], u.shape[1]
    f32 = jnp.float32
    u32 = u.astype(f32)
    ug = u32.reshape(bsz, seq, SSM_GROUPS, SSM_GROUP)
    y = (s5_scan(ug, lam_re[0], lam_im[0], log_dt[0], b_re[0], b_im[0], c_re[0], c_im[0], False)
         + s5_scan(ug, lam_re[1], lam_im[1], log_dt[1], b_re[1], b_im[1], c_re[1], c_im[1], True))
    y = y.reshape(bsz, seq, SSM_WIDTH) + d_skip.astype(f32) * u32
    a = jax.nn.gelu(y)
    val, gate = jnp.split(a @ w_glu.astype(f32), 2, axis=-1)
    return (val * jax.nn.sigmoid(gate)).astype(u.dtype)


def fnet_branch(u, w_fnet):
    bsz, seq = u.shape[0], u.shape[1]
    ug = u.astype(jnp.float32).reshape(bsz, seq, FFT_GROUPS, FFT_GROUP)
    mixed = jnp.fft.fft2(ug, axes=(1, 3), norm='ortho').real
    mixed = mixed.reshape(bsz, seq, FFT_WIDTH).astype(u.dtype)
    return mixed @ w_fnet


def setup_inputs(seed: int = 0) -> dict:
    key = jax.random.key(seed)
    ks = jax.random.split(key, 24)
    f32 = jnp.float32
    L = DEPTH

    def nrm(k, shape, scale):
        return jax.random.normal(k, shape, f32) * scale

    n_idx = jnp.arange(SSM_STATE, dtype=f32)
    lam_shape = (L, N_DIR, SSM_GROUPS, SSM_STATE)
    b_shape = (L, N_DIR, SSM_GROUPS, SSM_STATE, SSM_GROUP)
    c_shape = (L, N_DIR, SSM_GROUPS, SSM_GROUP, SSM_STATE)
    return {
        'x': nrm(ks[0], (BATCH, SEQ, D_MODEL), 1.0),
        'p': nrm(ks[1], (DEPTH, BATCH, SEQ, D_PLE), 1.0),
        'g_mix': 1.0 + nrm(ks[2], (L, D_MODEL), 0.02),
        'w_in': nrm(ks[3], (L, D_MODEL, IN_WIDTH), D_MODEL ** -0.5),
        'lam_re': -0.5 + nrm(ks[4], lam_shape, 0.01),
        'lam_im': math.pi * n_idx + nrm(ks[5], lam_shape, 0.01),
        'log_dt': jax.random.uniform(ks[6], (L, N_DIR, SSM_GROUPS), f32, math.log(DT_MIN), math.log(DT_MAX)),
        'b_re': nrm(ks[7], b_shape, (2 * SSM_GROUP) ** -0.5),
        'b_im': nrm(ks[8], b_shape, (2 * SSM_GROUP) ** -0.5),
        'c_re': nrm(ks[9], c_shape, SSM_STATE ** -0.5),
        'c_im': nrm(ks[10], c_shape, SSM_STATE ** -0.5),
        'd_skip': nrm(ks[11], (L, SSM_WIDTH), 1.0),
        'w_glu': nrm(ks[12], (L, SSM_WIDTH, 2 * D_MODEL), SSM_WIDTH ** -0.5),
        'w_fnet': nrm(ks[13], (L, FFT_WIDTH, D_MODEL), FFT_WIDTH ** -0.5),
        'w_out': nrm(ks[14], (L, D_MODEL, D_MODEL), D_MODEL ** -0.5),
        'g_ffn': 1.0 + nrm(ks[15], (L, D_MODEL), 0.02),
        'w_up': nrm(ks[16], (L, D_MODEL, D_FF), D_MODEL ** -0.5),
        'w_down': nrm(ks[17], (L, D_FF, D_MODEL), D_FF ** -0.5),
        'g_ple': 1.0 + nrm(ks[18], (L, D_MODEL), 0.02),
        'w_ple_gate': nrm(ks[19], (L, D_MODEL, D_MODEL), D_MODEL ** -0.5),
        'w_ple': nrm(ks[20], (L, D_PLE, D_MODEL), D_PLE ** -0.5),
        'g_final': 1.0 + nrm(ks[21], (D_MODEL,), 0.02),
    }


def reference(x, p, g_mix, w_in, lam_re, lam_im, log_dt, b_re, b_im, c_re, c_im, d_skip,
              w_glu, w_fnet, w_out, g_ffn, w_up, w_down, g_ple, w_ple_gate, w_ple, g_final):
    split_at = [SSM_WIDTH, SSM_WIDTH + FFT_WIDTH, SSM_WIDTH + FFT_WIDTH + D_MODEL]
    for i in range(DEPTH):
        h = rms_norm(x, g_mix[i])
        proj = h @ w_in[i]
        u_ssm, u_fft, z_a, z_b = jnp.split(proj, split_at, axis=-1)
        y_a = s5_branch(u_ssm, lam_re[i], lam_im[i], log_dt[i], b_re[i], b_im[i],
                        c_re[i], c_im[i], d_skip[i], w_glu[i])
        y_b = fnet_branch(u_fft, w_fnet[i])
        merged = jax.nn.sigmoid(z_a) * y_a + jax.nn.sigmoid(z_b) * y_b
        x = x + merged @ w_out[i]
        h = rms_norm(x, g_ffn[i])
        x = x + jnp.square(jax.nn.relu(h @ w_up[i])) @ w_down[i]
        h = rms_norm(x, g_ple[i])
        x = x + jax.nn.sigmoid(h @ w_ple_gate[i]) * (p[i] @ w_ple[i])
    return rms_norm(x, g_final)
```

```python
import contextlib
import math
import numpy as np
import ml_dtypes
import concourse.bass as bass
import concourse.mybir as mybir
from concourse.bass_utils import run_bass_kernel_spmd

F32 = mybir.dt.float32
BF16 = mybir.dt.bfloat16
I32 = mybir.dt.int32
AF = mybir.ActivationFunctionType
ALU = mybir.AluOpType

ENGS = ("pe", "act", "dve", "pool", "sp")
NCORES = 8
SEQ = 2048
D = 1024
TOK = 2 * SEQ
EPS = 1e-6
TWO_PI = 2.0 * math.pi
SIN2 = 6.28318
SIN1 = 3.14159


class Sched:
    def __init__(self, nc, n_dma_sems=32):
        self.nc = nc
        self.q = {e: [] for e in ENGS}
        self.cnt = {e: 0 for e in ENGS}
        self.sems = {}
        self.waited = {e: {} for e in ENGS}
        self.last_w = {}
        self.readers = {}
        self.n_dma = 0
        self.n_dma_sems = n_dma_sems
        self.dma_sem_cnt = [0] * n_dma_sems

    def alloc_sems(self, stack):
        for e in ("pe", "act", "dve", "pool"):
            self.sems[e] = stack.enter_context(self.nc.semaphore("s_" + e))
        for i in range(self.n_dma_sems):
            self.sems[("dma", i)] = stack.enter_context(self.nc.semaphore("s_dma%d" % i))

    def _need(self, eng, tok, waits):
        if tok is None:
            return
        key, val = tok
        if key == eng and eng == "pe":
            return
        if self.waited[eng].get(key, 0) >= val:
            return
        waits[key] = max(waits.get(key, 0), val)

    def op(self, eng, fn, reads=(), writes=(), dma=False):
        waits = {}
        for r in reads:
            self._need(eng, self.last_w.get(r), waits)
        for w in writes:
            self._need(eng, self.last_w.get(w), waits)
            for k, v in self.readers.get(w, {}).items():
                self._need(eng, (k, v), waits)
        for k, v in waits.items():
            self.waited[eng][k] = v
        wl = [(self.sems[k], v) for k, v in waits.items()]
        if dma:
            i = self.n_dma % self.n_dma_sems
            self.n_dma += 1
            self.dma_sem_cnt[i] += 16
            tok = (("dma", i), self.dma_sem_cnt[i])
            sem, inc = self.sems[("dma", i)], 16
        else:
            self.cnt[eng] += 1
            tok = (eng, self.cnt[eng])
            sem, inc = self.sems[eng], 1
        self.q[eng].append((wl, fn, sem, inc))
        for w in writes:
            self.last_w[w] = tok
            self.readers[w] = {}
        for r in reads:
            d = self.readers.setdefault(r, {})
            d[tok[0]] = max(d.get(tok[0], 0), tok[1])
        return tok

    def all_tokens(self):
        toks = [(e, self.cnt[e]) for e in ("pe", "act", "dve", "pool") if self.cnt[e] > 0]
        toks += [(("dma", i), c) for i, c in enumerate(self.dma_sem_cnt) if c > 0]
        return toks

    def barrier(self):
        toks = self.all_tokens()
        for e in ENGS:
            wl = []
            for k, v in toks:
                if k == e:
                    continue
                if self.waited[e].get(k, 0) >= v:
                    continue
                self.waited[e][k] = v
                wl.append((self.sems[k], v))
            if wl:
                self.q[e].append((wl, None, None, 0))

    def final_wait(self, eng):
        wl = [(self.sems[k], v) for k, v in self.all_tokens() if k != eng]
        self.q[eng].append((wl, None, None, 0))

    def emit(self, block):
        def run(name):
            def body(e):
                for wl, fn, sem, inc in self.q[name]:
                    for s, v in wl:
                        e.wait_ge(s, v)
                    if fn is not None:
                        fn(e).then_inc(sem, inc)
            return body
        block.tensor(run("pe"))
        block.scalar(run("act"))
        block.vector(run("dve"))
        block.gpsimd(run("pool"))
        block.sync(run("sp"))


def build(debug=False):
    nc = bass.Bass("TRN2", target_bir_lowering=False)
    S = Sched(nc)

    def din(name, shape, dt=F32):
        return nc.dram_tensor(name, list(shape), dt, kind="ExternalInput").ap()

    x = din("x", [TOK, D])
    p = din("p", [TOK, 256])
    g_mix = din("g_mix", [1, D]); g_ffn = din("g_ffn", [1, D]); g_ple = din("g_ple", [1, D])
    g_final = din("g_final", [D])
    w_in = din("w_in", [1, D, 3072])
    lam_re = din("lam_re", [1, 2, 32, 64]); lam_im = din("lam_im", [1, 2, 32, 64])
    log_dt = din("log_dt", [1, 2, 32])
    b_re = din("b_re", [1, 2, 32, 64, 16]); b_im = din("b_im", [1, 2, 32, 64, 16])
    c_re = din("c_re", [1, 2, 32, 16, 64]); c_im = din("c_im", [1, 2, 32, 16, 64])
    d_skip = din("d_skip", [1, 512])
    w_glu = din("w_glu", [1, 512, 2048]); w_fnet = din("w_fnet", [1, 512, 1024])
    w_out = din("w_out", [1, D, D]); w_up = din("w_up", [1, D, 4096]); w_down = din("w_down", [1, 4096, D])
    w_pg = din("w_ple_gate", [1, D, D]); w_ple = din("w_ple", [1, 256, D])
    c_ident = din("c_ident", [128, 128], BF16)
    c_ident32 = din("c_ident32", [128, 128])
    c_dftc = din("c_dftc", [128, 256], BF16)
    c_tbl = din("c_tbl", [2, SEQ, SEQ], BF16)
    c_mask = din("c_mask", [128, 8])
    c_tidx = din("c_tidx", [128, SEQ])
    out = nc.dram_tensor("out", [TOK, D], F32, kind="ExternalOutput").ap()
    mixD = nc.dram_tensor("mixD", [2, 4, 128, SEQ], BF16, kind="Internal").ap()
    aD = nc.dram_tensor("aD", [2, 4, 128, SEQ], BF16, kind="Internal").ap()
    x1D = nc.dram_tensor("x1D", [TOK, D], F32, kind="Internal").ap()

    def sb(name, shape, dt):
        return nc.alloc_sbuf_tensor(name, list(shape), dt)

    ident = sb("ident", [128, 128], BF16)
    ident32 = sb("ident32", [128, 128], F32)
    dftc = sb("dftc", [128, 256], BF16)
    maskc = sb("maskc", [128, 8], F32)
    gcols = sb("gcols", [128, 3, 8], F32)
    gfin = sb("gfin", [128, D], F32)
    mhalf = sb("mhalf", [128, 1], F32)
    small = sb("small", [128, 64], F32)
    ARENA_W = 199 * 256
    arena = sb("arena", [128, ARENA_W], F32)

    def view(off_kib, shape, dt):
        n = int(np.prod(shape))
        nbytes = n * (2 if dt == BF16 else 4)
        w0 = off_kib * 256
        w1 = w0 + (nbytes + 3) // 4
        assert w1 <= ARENA_W, (off_kib, shape)
        a = arena[:, w0:w1]
        if dt != F32:
            a = a.bitcast(dt)
        if len(shape) == 1:
            return a
        names = " ".join("a%d" % i for i in range(len(shape)))
        kw = {"a%d" % i: shape[i] for i in range(1, len(shape))}
        return a.rearrange("p (%s) -> p %s" % (names, names), **kw)

    banks = [nc.alloc_psum_tensor("bank%d" % i, [128, 512], F32) for i in range(8)]

    def bk(i):
        return banks[i][:]

    def bk16(i, shape):
        a = banks[i][:].bitcast(BF16)
        names = " ".join("a%d" % j for j in range(len(shape)))
        kw = {"a%d" % j: shape[j] for j in range(1, len(shape))}
        return a.rearrange("p (%s) -> p %s" % (names, names), **kw)

    def B(i):
        return "bank%d" % i

    def bcast_last(a, n):
        return bass.AP(a.tensor, a.offset, [list(t) for t in a.ap] + [[0, n]])

    uid = [0]

    def u(prefix):
        uid[0] += 1
        return "%s_%d" % (prefix, uid[0])

    dmaq = ["sp"]
    dq = [0]

    def dma(out_ap, in_ap, reads=(), writes=(), eng=None):
        if eng is None:
            eng = dmaq[dq[0] % len(dmaq)]
            dq[0] += 1
        return S.op(eng, lambda e: e.dma_start(out=out_ap, in_=in_ap), reads=reads, writes=writes, dma=True)

    with contextlib.ExitStack() as st:
        S.alloc_sems(st)
        st.enter_context(nc.allow_non_contiguous_dma(reason="small parameter relayouts"))
        block = st.enter_context(nc.Block())

        dma(ident[:], c_ident, writes=["ident"])
        dma(ident32[:], c_ident32, writes=["ident32"])
        dma(dftc[:], c_dftc, writes=["dftc"])
        dma(maskc[:], c_mask, writes=["maskc"])
        for i, g in enumerate((g_mix, g_ffn, g_ple)):
            dma(gcols[:, i, :], g[0].rearrange("(k p) -> p k", p=128), writes=["gcols"])
        dma(gfin[:], g_final.partition_broadcast(128), writes=["gfin"])
        S.op("pool", lambda e: e.memset(mhalf[:], -0.5), writes=["mhalf"])

        BTf = view(0, [32, 2, 128], BF16)
        BTb = view(16, [32, 2, 128], BF16)
        Cm = view(32, [32, 2, 128], BF16)
        Dmat = view(48, [4, 128], BF16)
        Dcol = view(49, [32], F32)
        Rcol = view(50, [32], F32)
        usT = view(52, [4, 2, SEQ], BF16)
        wu = view(84, [8, 1024], BF16)
        V = view(100, [16, 2, 4, 128], BF16)
        TMP = 132

        pp0 = TMP
        lre = view(pp0, [32], F32); lim = view(pp0 + 1, [32], F32); ldt = view(pp0 + 2, [32], F32)
        thf = view(pp0 + 3, [32], F32); ti_ = view(pp0 + 4, [32], I32); tk = view(pp0 + 5, [32], F32)
        s1 = view(pp0 + 6, [32], F32); h1 = view(pp0 + 7, [32], F32); c1 = view(pp0 + 8, [32], F32)
        ar = view(pp0 + 9, [32], F32); ai = view(pp0 + 10, [32], F32); l2 = view(pp0 + 11, [32], F32)
        kre = view(pp0 + 12, [32], F32); kim = view(pp0 + 13, [32], F32); tq = view(pp0 + 14, [32], F32)
        tq2 = view(pp0 + 15, [32], F32)
        Bre = view(pp0 + 16, [32, 16], F32); Bim = view(pp0 + 18, [32, 16], F32)
        Cre = view(pp0 + 20, [32, 16], F32); Cim = view(pp0 + 22, [32, 16], F32)
        Bbr = view(pp0 + 24, [32, 16], F32); Bbi = view(pp0 + 26, [32, 16], F32)
        tB1 = view(pp0 + 28, [32, 16], F32); tB2 = view(pp0 + 30, [32, 16], F32)
        dcol = view(pp0 + 32, [4], F32)

        for d in range(2):
            ps = slice(64 * d, 64 * d + 64)
            dma(lre[ps, :], lam_re[0, d].rearrange("g p -> p g"), writes=["lre"])
            dma(lim[ps, :], lam_im[0, d].rearrange("g p -> p g"), writes=["lim"])
            dma(ldt[ps, :], log_dt[0, d].partition_broadcast(64), writes=["ldt"])
            dma(Bre[ps], b_re[0, d].rearrange("g p h -> p g h"), writes=["Bre"])
            dma(Bim[ps], b_im[0, d].rearrange("g p h -> p g h"), writes=["Bim"])
            for gq in range(4):
                gs = slice(8 * gq, 8 * gq + 8)
                dma(Cre[ps, gs, :], c_re[0, d, gs].rearrange("g h p -> p g h"), writes=["Cre"], eng="pool")
                dma(Cim[ps, gs, :], c_im[0, d, gs].rearrange("g h p -> p g h"), writes=["Cim"], eng="pool")
        dma(dcol[:], d_skip[0].rearrange("(c p) -> p c", p=128), writes=["dcol"])

        V_ = "dve"
        S.op("act", lambda e: e.activation(out=ldt[:], in_=ldt[:], func=AF.Exp), reads=["ldt"], writes=["ldt"])
        S.op(V_, lambda e: e.tensor_tensor(out=tq[:], in0=lre[:], in1=ldt[:], op=ALU.mult), reads=["lre", "ldt"], writes=["tq"])
        S.op("act", lambda e: e.activation(out=Dcol[:], in_=tq[:], func=AF.Exp), reads=["tq"], writes=["Dcol"])
        S.op(V_, lambda e: e.tensor_tensor(out=thf[:], in0=lim[:], in1=ldt[:], op=ALU.mult), reads=["lim", "ldt"], writes=["thf"])
        S.op(V_, lambda e: e.tensor_scalar(out=thf[:], in0=thf[:], scalar1=1.0 / TWO_PI, scalar2=None, op0=ALU.mult), reads=["thf"], writes=["thf"])
        S.op(V_, lambda e: e.tensor_copy(out=ti_[:], in_=thf[:]), reads=["thf"], writes=["ti"])
        S.op(V_, lambda e: e.tensor_copy(out=tk[:], in_=ti_[:]), reads=["ti"], writes=["tk"])
        S.op(V_, lambda e: e.tensor_tensor(out=Rcol[:], in0=thf[:], in1=tk[:], op=ALU.subtract), reads=["thf", "tk"], writes=["Rcol"])
        S.op("act", lambda e: e.activation(out=s1[:], in_=Rcol[:], func=AF.Sin, scale=SIN2), reads=["Rcol"], writes=["s1"])
        S.op("act", lambda e: e.activation(out=h1[:], in_=Rcol[:], func=AF.Sin, scale=SIN1), reads=["Rcol"], writes=["h1"])
        S.op(V_, lambda e: e.tensor_tensor(out=h1[:], in0=h1[:], in1=h1[:], op=ALU.mult), reads=["h1"], writes=["h1"])
        S.op(V_, lambda e: e.tensor_scalar(out=c1[:], in0=h1[:], scalar1=-2.0, scalar2=1.0, op0=ALU.mult, op1=ALU.add), reads=["h1"], writes=["c1"])
        S.op(V_, lambda e: e.tensor_tensor(out=ar[:], in0=Dcol[:], in1=c1[:], op=ALU.mult), reads=["Dcol", "c1"], writes=["ar"])
        S.op(V_, lambda e: e.tensor_scalar(out=ar[:], in0=ar[:], scalar1=-1.0, scalar2=None, op0=ALU.add), reads=["ar"], writes=["ar"])
        S.op(V_, lambda e: e.tensor_tensor(out=ai[:], in0=Dcol[:], in1=s1[:], op=ALU.mult), reads=["Dcol", "s1"], writes=["ai"])
        S.op(V_, lambda e: e.tensor_tensor(out=l2[:], in0=lre[:], in1=lre[:], op=ALU.mult), reads=["lre"], writes=["l2"])
        S.op(V_, lambda e: e.tensor_tensor(out=tq[:], in0=lim[:], in1=lim[:], op=ALU.mult), reads=["lim"], writes=["tq"])
        S.op(V_, lambda e: e.tensor_tensor(out=l2[:], in0=l2[:], in1=tq[:], op=ALU.add), reads=["l2", "tq"], writes=["l2"])
        S.op(V_, lambda e: e.reciprocal(out=l2[:], in_=l2[:]), reads=["l2"], writes=["l2"])
        S.op(V_, lambda e: e.tensor_tensor(out=kre[:], in0=ar[:], in1=lre[:], op=ALU.mult), reads=["ar", "lre"], writes=["kre"])
        S.op(V_, lambda e: e.tensor_tensor(out=tq[:], in0=ai[:], in1=lim[:], op=ALU.mult), reads=["ai", "lim"], writes=["tq"])
        S.op(V_, lambda e: e.tensor_tensor(out=kre[:], in0=kre[:], in1=tq[:], op=ALU.add), reads=["kre", "tq"], writes=["kre"])
        S.op(V_, lambda e: e.tensor_tensor(out=kre[:], in0=kre[:], in1=l2[:], op=ALU.mult), reads=["kre", "l2"], writes=["kre"])
        S.op(V_, lambda e: e.tensor_tensor(out=kim[:], in0=ai[:], in1=lre[:], op=ALU.mult), reads=["ai", "lre"], writes=["kim"])
        S.op(V_, lambda e: e.tensor_tensor(out=tq2[:], in0=ar[:], in1=lim[:], op=ALU.mult), reads=["ar", "lim"], writes=["tq2"])
        S.op(V_, lambda e: e.tensor_tensor(out=kim[:], in0=kim[:], in1=tq2[:], op=ALU.subtract), reads=["kim", "tq2"], writes=["kim"])
        S.op(V_, lambda e: e.tensor_tensor(out=kim[:], in0=kim[:], in1=l2[:], op=ALU.mult), reads=["kim", "l2"], writes=["kim"])
        kre_b = bcast_last(kre[:], 16)
        kim_b = bcast_last(kim[:], 16)
        S.op(V_, lambda e: e.tensor_tensor(out=Bbr[:], in0=Bre[:], in1=kre_b, op=ALU.mult), reads=["Bre", "kre"], writes=["Bbr"])
        S.op(V_, lambda e: e.tensor_tensor(out=tB1[:], in0=Bim[:], in1=kim_b, op=ALU.mult), reads=["Bim", "kim"], writes=["tB1"])
        S.op(V_, lambda e: e.tensor_tensor(out=Bbr[:], in0=Bbr[:], in1=tB1[:], op=ALU.subtract), reads=["Bbr", "tB1"], writes=["Bbr"])
        S.op(V_, lambda e: e.tensor_tensor(out=Bbi[:], in0=Bre[:], in1=kim_b, op=ALU.mult), reads=["Bre", "kim"], writes=["Bbi"])
        S.op(V_, lambda e: e.tensor_tensor(out=tB2[:], in0=Bim[:], in1=kre_b, op=ALU.mult), reads=["Bim", "kre"], writes=["tB2"])
        S.op(V_, lambda e: e.tensor_tensor(out=Bbi[:], in0=Bbi[:], in1=tB2[:], op=ALU.add), reads=["Bbi", "tB2"], writes=["Bbi"])
        S.op("pool", lambda e: e.memset(BTf[:], 0.0), writes=["BTf"])
        S.op("pool", lambda e: e.memset(BTb[:], 0.0), writes=["BTb"])
        S.op("pool", lambda e: e.memset(Cm[:], 0.0), writes=["Cm"])
        for ri, Bb in enumerate((Bbr, Bbi)):
            for ct in range(4):
                bi = 6 + (ct % 2)
                S.op("pe", lambda e, Bb=Bb, ct=ct, bi=bi: e.transpose(out=bk(bi)[:, 0:128], in_=Bb[:, 8 * ct:8 * ct + 8, :].rearrange("p g h -> p (g h)"), identity=ident32[:]),
                     reads=["Bbr", "Bbi", "ident32"], writes=[B(bi)])
                for gl in range(8):
                    g = 8 * ct + gl
                    S.op(V_, lambda e, g=g, gl=gl, ri=ri, bi=bi: e.tensor_scalar(out=BTf[:, g, ri, 0:64], in0=bk(bi)[:, 0:64], scalar1=maskc[:, gl:gl + 1], scalar2=None, op0=ALU.mult),
                         reads=[B(bi), "maskc"], writes=["BTf"])
                    S.op(V_, lambda e, g=g, gl=gl, ri=ri, bi=bi: e.tensor_scalar(out=BTb[:, g, ri, 64:128], in0=bk(bi)[:, 64:128], scalar1=maskc[:, gl:gl + 1], scalar2=None, op0=ALU.mult),
                         reads=[B(bi), "maskc"], writes=["BTb"])
        for gl in range(8):
            S.op("act", lambda e, gl=gl: e.activation(out=Cm[:, gl::8, 0, 16 * gl:16 * gl + 16], in_=Cre[:, gl::8, :], func=AF.Copy),
                 reads=["Cre"], writes=["Cm"])
            S.op("act", lambda e, gl=gl: e.activation(out=Cm[:, gl::8, 1, 16 * gl:16 * gl + 16], in_=Cim[:, gl::8, :], func=AF.Copy, scale=-1.0),
                 reads=["Cim"], writes=["Cm"])
        for ct in range(4):
            S.op(V_, lambda e, ct=ct: e.tensor_scalar(out=Dmat[:, ct, :], in0=ident32[:], scalar1=dcol[:, ct:ct + 1], scalar2=None, op0=ALU.mult),
                 reads=["ident32", "dcol"], writes=["Dmat"])
        S.barrier()

        def load_scaled(dst, src_rows, K, N, gi, stg_off):
            stg = [view(stg_off, [N], F32), view(stg_off + N // 256, [N], F32)]
            for k in range(K):
                b = k % 2
                dma(stg[b][:], src_rows[k * 128:(k + 1) * 128, :], writes=["stg%d" % b])
                eng = "act" if k % 2 == 0 else "dve"
                if eng == "act":
                    S.op("act", lambda e, k=k, b=b: e.activation(out=dst[:, k, :], in_=stg[b][:], func=AF.Copy, scale=gcols[:, gi, k:k + 1]),
                         reads=["stg%d" % b, "gcols"], writes=["wdst"])
                else:
                    S.op("dve", lambda e, k=k, b=b: e.tensor_scalar(out=dst[:, k, :], in0=stg[b][:], scalar1=gcols[:, gi, k:k + 1], scalar2=None, op0=ALU.mult),
                         reads=["stg%d" % b, "gcols"], writes=["wdst"])

        def load_cast(dst, src_rows):
            return dma(dst, src_rows.rearrange("(k p) n -> p k n", p=128), writes=["wdst"], eng="pool")

        load_scaled(wu, w_in[0][:, 0:1024], 8, 1024, 0, TMP)
        S.barrier()

        xtA = [view(TMP + 4 * i, [D], F32) for i in range(3)]
        junkA = view(TMP + 12, [D], BF16)
        xn = [view(TMP + 14 + 2 * i, [D], BF16) for i in range(2)]
        hT = [view(TMP + 18 + 8 * i, [8, 512], BF16) for i in range(2)]
        ufT = view(TMP + 34, [4, 512], BF16)
        tblb = [view(TMP + 8 * i, [8, 512], BF16) for i in range(4)]
        mx = [view(TMP + 32 + i, [512], BF16) for i in range(2)]
        mixst = view(TMP + 34, [4, 512], BF16)
        sc = [0]

        def rms_stats(xtile, xres, slot, junk):
            ssv = small[:, slot:slot + 1]
            S.op("act", lambda e: e.activation(out=junk[:], in_=xtile, func=AF.Square, accum_out=ssv), reads=[xres], writes=["junk", "sm%d" % slot])
            S.op("dve", lambda e: e.tensor_scalar(out=ssv, in0=ssv, scalar1=1.0 / D, scalar2=EPS, op0=ALU.mult, op1=ALU.add), reads=["sm%d" % slot], writes=["sm%d" % slot])
            S.op("pool", lambda e: e.tensor_tensor(out=ssv, in0=ssv, in1=mhalf[:], op=ALU.pow), reads=["sm%d" % slot, "mhalf"], writes=["sm%d" % slot])
            return ssv

        ti_ctr = [0]
        for seq in range(2):
            for j in range(4):
                hb = j % 2
                for i in range(4):
                    tix = ti_ctr[0]; ti_ctr[0] += 1
                    xb = tix % 3
                    r0 = seq * SEQ + j * 512 + i * 128
                    dma(xtA[xb][:], x[r0:r0 + 128, :], writes=["xt%d" % xb])
                    slot = 2 * (tix % 8)
                    rstd = rms_stats(xtA[xb][:], "xt%d" % xb, slot, junkA)
                    nb = tix % 2
                    S.op("dve", lambda e, xb=xb, nb=nb, rstd=rstd: e.tensor_scalar(out=xn[nb][:], in0=xtA[xb][:], scalar1=rstd, scalar2=None, op0=ALU.mult),
                         reads=["xt%d" % xb, "sm%d" % slot], writes=["xn%d" % nb])
                    tb = 6 + (tix % 2)
                    for k in range(8):
                        S.op("pe", lambda e, k=k, nb=nb, tb=tb: e.transpose(out=bk16(tb, [8, 128])[:, k, :], in_=xn[nb][:, k * 128:(k + 1) * 128], identity=ident[:]),
                             reads=["xn%d" % nb, "ident"], writes=[B(tb)])
                    S.op("act", lambda e, hb=hb, i=i, tb=tb: e.activation(out=hT[hb][:, :, i * 128:(i + 1) * 128], in_=bk16(tb, [8, 128]), func=AF.Copy),
                         reads=[B(tb)], writes=["hT%d" % hb])
                for m in range(8):
                    ub = m % 4
                    for k in range(8):
                        S.op("pe", lambda e, m=m, k=k, ub=ub, hb=hb: e.matmul(bk(ub), lhsT=wu[:, k, m * 128:(m + 1) * 128], rhs=hT[hb][:, k, :], start=(k == 0), stop=(k == 7)),
                             reads=["wdst", "hT%d" % hb], writes=[B(ub)])
                    if m < 4:
                        eng = "act" if m % 2 == 0 else "dve"
                        if eng == "act":
                            S.op("act", lambda e, m=m, ub=ub, seq=seq, j=j: e.activation(out=usT[:, m, seq, j * 512:(j + 1) * 512], in_=bk(ub), func=AF.Copy),
                                 reads=[B(ub)], writes=["usT"])
                        else:
                            S.op("dve", lambda e, m=m, ub=ub, seq=seq, j=j: e.tensor_copy(out=usT[:, m, seq, j * 512:(j + 1) * 512], in_=bk(ub)),
                                 reads=[B(ub)], writes=["usT"])
                    else:
                        q = m - 4
                        eng = "act" if m % 2 == 0 else "dve"
                        if eng == "act":
                            S.op("act", lambda e, q=q, ub=ub: e.activation(out=ufT[:, q, :], in_=bk(ub), func=AF.Copy), reads=[B(ub)], writes=["ufT%d" % q])
                        else:
                            S.op("dve", lambda e, q=q, ub=ub: e.tensor_copy(out=ufT[:, q, :], in_=bk(ub)), reads=[B(ub)], writes=["ufT%d" % q])
                for i in range(4):
                    kk = j * 4 + i
                    for q in range(4):
                        bi = 4 + q // 2
                        S.op("pe", lambda e, q=q, i=i, bi=bi: e.matmul(bk(bi)[:, (q % 2) * 256:(q % 2) * 256 + 256], lhsT=ufT[:, q, i * 128:(i + 1) * 128], rhs=dftc[:], start=True, stop=True),
                             reads=["ufT%d" % q, "dftc"], writes=[B(bi)])
                    for half in range(2):
                        bi = 4 + half
                        src = bk(bi).rearrange("p (q cs c) -> p cs q c", q=2, cs=2)
                        dst = V[:, kk, :, 2 * half:2 * half + 2, :]
                        if half == 0:
                            S.op("act", lambda e, src=src, dst=dst: e.activation(out=dst, in_=src, func=AF.Copy), reads=[B(bi)], writes=["V"])
                        else:
                            S.op("dve", lambda e, src=src, dst=dst: e.tensor_copy(out=dst, in_=src), reads=[B(bi)], writes=["V"])
            S.barrier()
            tcn = [0]
            for tg in range(4):
                for cidx, (cs, kh) in enumerate(((0, 0), (0, 1), (1, 0), (1, 1))):
                    b = tcn[0] % 4; tcn[0] += 1
                    dma(tblb[b][:], c_tbl[cs, kh * 1024:(kh + 1) * 1024, tg * 512:(tg + 1) * 512].rearrange("(k p) t -> p k t", p=128), writes=["tbl%d" % b])
                    for ti in range(4):
                        for k in range(8):
                            S.op("pe", lambda e, b=b, ti=ti, k=k, cs=cs, kh=kh, cidx=cidx: e.matmul(
                                bk(ti), lhsT=tblb[b][:, k, ti * 128:(ti + 1) * 128], rhs=V[:, kh * 8 + k, cs, :, :],
                                start=(cidx == 0 and k == 0), stop=(cidx == 3 and k == 7)),
                                reads=["tbl%d" % b, "V"], writes=[B(ti)])
                for ti in range(4):
                    mb = ti % 2
                    S.op("act", lambda e, ti=ti, mb=mb: e.activation(out=mx[mb][:], in_=bk(ti), func=AF.Copy), reads=[B(ti)], writes=["mx%d" % mb])
                    tb = 6 + (ti % 2)
                    for q in range(4):
                        S.op("pe", lambda e, q=q, mb=mb, tb=tb: e.transpose(out=bk16(tb, [8, 128])[:, q, :], in_=mx[mb][:, q * 128:(q + 1) * 128], identity=ident[:]),
                             reads=["mx%d" % mb, "ident"], writes=[B(tb)])
                    S.op("dve", lambda e, ti=ti, tb=tb: e.tensor_copy(out=mixst[:, :, ti * 128:(ti + 1) * 128], in_=bk16(tb, [8, 128])[:, 0:4, :]),
                         reads=[B(tb)], writes=["mixst"])
                dma(mixD[seq].rearrange("q c t -> c q t")[:, :, tg * 512:(tg + 1) * 512], mixst[:], reads=["mixst"], writes=["mixD"])
            S.barrier()

        S0 = 84
        tidx = view(S0, [SEQ], F32)
        tt = view(S0 + 8, [SEQ], F32)
        tki = view(S0 + 16, [SEQ], I32)
        sn = view(S0 + 24, [SEQ], F32)
        tkf = sn
        cs_ = view(S0 + 32, [SEQ], F32)
        Sre = view(S0 + 40, [SEQ], F32); Sim = view(S0 + 48, [SEQ], F32)
        Kre = view(S0 + 56, [SEQ], F32); Kim = view(S0 + 64, [SEQ], F32)
        t1 = view(S0 + 72, [512], F32); t2 = view(S0 + 74, [512], F32)
        Hraw = view(S0 + 76, [2, SEQ], BF16)
        Hh = view(S0 + 84, [2, SEQ], BF16)
        urev_t = view(176, [SEQ], BF16)
        aTs = view(180, [SEQ], BF16)
        u1 = view(184, [512], F32)
        u2 = view(186, [512], F32)
        dma(tidx[:], c_tidx, writes=["tidx"])

        for seq in range(2):
            for ct in range(4):
                S.op("dve", lambda e, ct=ct, seq=seq: e.tensor_copy(out=urev_t[:], in_=usT[:, ct, seq, ::-1]), reads=["usT"], writes=["urev"])
                for ch in range(4):
                    S.op("pe", lambda e, ct=ct, seq=seq, ch=ch: e.matmul(bk(ch), lhsT=Dmat[:, ct, :], rhs=usT[:, ct, seq, ch * 512:(ch + 1) * 512], start=True, stop=False),
                         reads=["Dmat", "usT"], writes=[B(ch)])
                for gl in range(8):
                    g = 8 * ct + gl
                    S.op("pool", lambda e, g=g: e.tensor_scalar(out=tt[:], in0=tidx[:], scalar1=Rcol[:, g:g + 1], scalar2=1.0, op0=ALU.mult, op1=ALU.mult), reads=["tidx", "Rcol"], writes=["tt"])
                    S.op("dve", lambda e: e.tensor_copy(out=tki[:], in_=tt[:]), reads=["tt"], writes=["tki"])
                    S.op("dve", lambda e: e.tensor_copy(out=tkf[:], in_=tki[:]), reads=["tki"], writes=["sn"])
                    S.op("pool", lambda e: e.tensor_tensor(out=tt[:], in0=tt[:], in1=tkf[:], op=ALU.subtract), reads=["tt", "sn"], writes=["tt"])
                    S.op("act", lambda e: e.activation(out=sn[:], in_=tt[:], func=AF.Sin, scale=SIN2), reads=["tt"], writes=["sn"])
                    S.op("act", lambda e: e.activation(out=cs_[:], in_=tt[:], func=AF.Sin, scale=SIN1), reads=["tt"], writes=["cs"])
                    S.op("act", lambda e: e.activation(out=cs_[:], in_=cs_[:], func=AF.Square), reads=["cs"], writes=["cs"])
                    S.op("act", lambda e: e.activation(out=cs_[:], in_=cs_[:], func=AF.Identity, scale=-2.0, bias=1.0), reads=["cs"], writes=["cs"])
                    for ch in range(4):
                        csl = slice(ch * 512, (ch + 1) * 512)
                        bb = 4 + 2 * (ch % 2)
                        for ri in range(2):
                            S.op("pe", lambda e, g=g, ri=ri, bb=bb, ct=ct, seq=seq, csl=csl: e.matmul(bk(bb + ri), lhsT=BTf[:, g, ri, :], rhs=usT[:, ct, seq, csl], start=True, stop=False),
                                 reads=["BTf", "usT"], writes=[B(bb + ri)])
                            S.op("pe", lambda e, g=g, ri=ri, bb=bb, csl=csl: e.matmul(bk(bb + ri), lhsT=BTb[:, g, ri, :], rhs=urev_t[:, csl], start=False, stop=True),
                                 reads=["BTb", "urev"], writes=[B(bb + ri)])
                        S.op("dve", lambda e, bb=bb, csl=csl: e.tensor_tensor(out=t1[:], in0=bk(bb), in1=cs_[:, csl], op=ALU.mult), reads=[B(bb), "cs"], writes=["t1"])
                        S.op("dve", lambda e, bb=bb, csl=csl: e.tensor_tensor(out=t2[:], in0=bk(bb + 1), in1=sn[:, csl], op=ALU.mult), reads=[B(bb + 1), "sn"], writes=["t2"])
                        S.op("dve", lambda e, csl=csl: e.tensor_tensor(out=Sre[:, csl], in0=t1[:], in1=t2[:], op=ALU.add), reads=["t1", "t2"], writes=["Sre"])
                        S.op("dve", lambda e, bb=bb, csl=csl: e.tensor_tensor(out=t1[:], in0=bk(bb + 1), in1=cs_[:, csl], op=ALU.mult), reads=[B(bb + 1), "cs"], writes=["t1"])
                        S.op("dve", lambda e, bb=bb, csl=csl: e.tensor_tensor(out=t2[:], in0=bk(bb), in1=sn[:, csl], op=ALU.mult), reads=[B(bb), "sn"], writes=["t2"])
                        S.op("dve", lambda e, csl=csl: e.tensor_tensor(out=Sim[:, csl], in0=t1[:], in1=t2[:], op=ALU.subtract), reads=["t1", "t2"], writes=["Sim"])
                    dec_b = Dcol[:, g:g + 1].to_broadcast([128, SEQ])
                    S.op("dve", lambda e, dec_b=dec_b: e.tensor_tensor_scan(out=Kre[:], data0=dec_b, data1=Sre[:], initial=0.0, op0=ALU.mult, op1=ALU.add), reads=["Sre", "Dcol"], writes=["Kre"])
                    S.op("dve", lambda e, dec_b=dec_b: e.tensor_tensor_scan(out=Kim[:], data0=dec_b, data1=Sim[:], initial=0.0, op0=ALU.mult, op1=ALU.add), reads=["Sim", "Dcol"], writes=["Kim"])
                    for ch in range(4):
                        csl = slice(ch * 512, (ch + 1) * 512)
                        S.op("pool", lambda e, csl=csl: e.tensor_tensor(out=u1[:], in0=Kre[:, csl], in1=cs_[:, csl], op=ALU.mult), reads=["Kre", "cs"], writes=["u1"])
                        S.op("pool", lambda e, csl=csl: e.tensor_tensor(out=u2[:], in0=Kim[:, csl], in1=sn[:, csl], op=ALU.mult), reads=["Kim", "sn"], writes=["u2"])
                        S.op("pool", lambda e, csl=csl: e.tensor_tensor(out=Hraw[:, 0, csl], in0=u1[:], in1=u2[:], op=ALU.subtract), reads=["u1", "u2"], writes=["Hraw0"])
                        S.op("pool", lambda e, csl=csl: e.tensor_tensor(out=u1[:], in0=Kim[:, csl], in1=cs_[:, csl], op=ALU.mult), reads=["Kim", "cs"], writes=["u1"])
                        S.op("pool", lambda e, csl=csl: e.tensor_tensor(out=u2[:], in0=Kre[:, csl], in1=sn[:, csl], op=ALU.mult), reads=["Kre", "sn"], writes=["u2"])
                        S.op("pool", lambda e, csl=csl: e.tensor_tensor(out=Hraw[:, 1, csl], in0=u1[:], in1=u2[:], op=ALU.add), reads=["u1", "u2"], writes=["Hraw1"])
                    for ri in range(2):
                        S.op("dve", lambda e, ri=ri: e.tensor_copy(out=Hh[0:64, ri, :], in_=Hraw[0:64, ri, :]), reads=["Hraw%d" % ri], writes=["Hf%d" % ri])
                        S.op("dve", lambda e, ri=ri: e.tensor_copy(out=Hh[64:128, ri, :], in_=Hraw[64:128, ri, ::-1]), reads=["Hraw%d" % ri], writes=["Hb%d" % ri])
                    for ch in range(4):
                        csl = slice(ch * 512, (ch + 1) * 512)
                        for ri in range(2):
                            last = (gl == 7 and ri == 1)
                            S.op("pe", lambda e, g=g, ri=ri, ch=ch, csl=csl, last=last: e.matmul(bk(ch), lhsT=Cm[:, g, ri, :], rhs=Hh[:, ri, csl], start=False, stop=last),
                                 reads=["Cm", "Hf%d" % ri, "Hb%d" % ri], writes=[B(ch)])
                for ch in range(4):
                    S.op("act", lambda e, ch=ch: e.activation(out=aTs[:, ch * 512:(ch + 1) * 512], in_=bk(ch), func=AF.Gelu_apprx_tanh), reads=[B(ch)], writes=["aTs"])
                dma(aD[seq, ct], aTs[:], reads=["aTs"], writes=["aD"])
        S.barrier()

        wz = view(0, [8, 2048], BF16)
        wgl = view(32, [4, 2048], BF16)
        wfn = view(48, [4, 1024], BF16)
        wo = view(56, [8, 1024], BF16)
        load_scaled(wz, w_in[0][:, 1024:3072], 8, 2048, 0, 132)
        load_cast(wgl[:], w_glu[0])
        load_cast(wfn[:], w_fnet[0])
        load_cast(wo[:], w_out[0])
        C0 = 72
        xtC = [view(C0 + 4 * i, [D], F32) for i in range(3)]
        junkC = view(C0 + 12, [D], BF16)
        xbC = [view(C0 + 14 + 2 * i, [D], BF16) for i in range(2)]
        hTt = [view(C0 + 18 + 2 * i, [8, 128], BF16) for i in range(2)]
        aTj = [view(C0 + 22 + 4 * i, [4, 512], BF16) for i in range(2)]
        mTj = [view(C0 + 30 + 4 * i, [4, 512], BF16) for i in range(2)]
        sg = view(C0 + 38, [2048], BF16)
        sgate = view(C0 + 42, [1024], BF16)
        ya = view(C0 + 44, [1024], F32)
        t2c = view(C0 + 48, [1024], F32)
        mg = view(C0 + 52, [1024], BF16)
        mgT = view(C0 + 54, [8, 128], BF16)
        x1t = [view(C0 + 56 + 4 * i, [D], F32) for i in range(2)]
        S.barrier()
        mmb = [0]

        def nb():
            b = mmb[0] % 6
            mmb[0] += 1
            return b

        tix = 0
        for jj in range(8):
            seq, j = jj // 4, jj % 4
            ab = jj % 2
            dma(aTj[ab][:], aD[seq].rearrange("c p t -> p c t")[:, :, j * 512:(j + 1) * 512], reads=["aD"], writes=["aTj%d" % ab])
            dma(mTj[ab][:], mixD[seq].rearrange("q c t -> c q t")[:, :, j * 512:(j + 1) * 512], reads=["mixD"], writes=["mTj%d" % ab])
            for i in range(4):
                xb = tix % 3; hb = tix % 2
                r0 = jj * 512 + i * 128
                tsl = slice(i * 128, (i + 1) * 128)
                dma(xtC[xb][:], x[r0:r0 + 128, :], writes=["xt%d" % xb])
                slot = 2 * (tix % 8)
                rstd = rms_stats(xtC[xb][:], "xt%d" % xb, slot, junkC)
                S.op("pool", lambda e, xb=xb, hb=hb: e.tensor_copy(out=xbC[hb][:], in_=xtC[xb][:]), reads=["xt%d" % xb], writes=["xb16%d" % hb])
                tb = 6 + (tix % 2)
                for k in range(8):
                    S.op("pe", lambda e, k=k, hb=hb, tb=tb: e.transpose(out=bk16(tb, [8, 128])[:, k, :], in_=xbC[hb][:, k * 128:(k + 1) * 128], identity=ident[:]),
                         reads=["xb16%d" % hb, "ident"], writes=[B(tb)])
                S.op("act", lambda e, hb=hb, tb=tb: e.activation(out=hTt[hb][:], in_=bk16(tb, [8, 128]), func=AF.Copy), reads=[B(tb)], writes=["hTt%d" % hb])
                for c in range(4):
                    b = nb()
                    for k in range(8):
                        S.op("pe", lambda e, b=b, c=c, k=k, hb=hb: e.matmul(bk(b), lhsT=hTt[hb][:, k, :], rhs=wz[:, k, c * 512:(c + 1) * 512], start=(k == 0), stop=(k == 7)),
                             reads=["hTt%d" % hb, "wdst"], writes=[B(b)])
                    S.op("act", lambda e, b=b, c=c, rstd=rstd: e.activation(out=sg[:, c * 512:(c + 1) * 512], in_=bk(b), func=AF.Sigmoid, scale=rstd),
                         reads=[B(b), "sm%d" % slot], writes=["sg%d" % c])
                gbanks = []
                for c in range(4):
                    b = nb(); gbanks.append(b)
                    for k in range(4):
                        S.op("pe", lambda e, b=b, c=c, k=k, ab=ab, tsl=tsl: e.matmul(bk(b), lhsT=aTj[ab][:, k, tsl], rhs=wgl[:, k, c * 512:(c + 1) * 512], start=(k == 0), stop=(k == 3)),
                             reads=["aTj%d" % ab, "wdst"], writes=[B(b)])
                for c in range(2):
                    S.op("act", lambda e, c=c, b=gbanks[2 + c]: e.activation(out=sgate[:, c * 512:(c + 1) * 512], in_=bk(b), func=AF.Sigmoid),
                         reads=[B(gbanks[2 + c])], writes=["sgate%d" % c])
                    S.op("dve", lambda e, c=c, b=gbanks[c]: e.tensor_tensor(out=ya[:, c * 512:(c + 1) * 512], in0=bk(b), in1=sgate[:, c * 512:(c + 1) * 512], op=ALU.mult),
                         reads=[B(gbanks[c]), "sgate%d" % c], writes=["ya%d" % c])
                    S.op("pool", lambda e, c=c: e.tensor_tensor(out=ya[:, c * 512:(c + 1) * 512], in0=ya[:, c * 512:(c + 1) * 512], in1=sg[:, c * 512:(c + 1) * 512], op=ALU.mult),
                         reads=["ya%d" % c, "sg%d" % c], writes=["ya%d" % c])
                for c in range(2):
                    b = nb()
                    for k in range(4):
                        S.op("pe", lambda e, b=b, c=c, k=k, ab=ab, tsl=tsl: e.matmul(bk(b), lhsT=mTj[ab][:, k, tsl], rhs=wfn[:, k, c * 512:(c + 1) * 512], start=(k == 0), stop=(k == 3)),
                             reads=["mTj%d" % ab, "wdst"], writes=[B(b)])
                    S.op("dve", lambda e, b=b, c=c: e.tensor_tensor(out=t2c[:, c * 512:(c + 1) * 512], in0=bk(b), in1=sg[:, 1024 + c * 512:1024 + (c + 1) * 512], op=ALU.mult),
                         reads=[B(b), "sg%d" % (2 + c)], writes=["t2c%d" % c])
                    S.op("pool", lambda e, c=c: e.tensor_tensor(out=mg[:, c * 512:(c + 1) * 512], in0=ya[:, c * 512:(c + 1) * 512], in1=t2c[:, c * 512:(c + 1) * 512], op=ALU.add),
                         reads=["ya%d" % c, "t2c%d" % c], writes=["mg"])
                tb2 = 6 + ((tix + 1) % 2)
                for k in range(8):
                    S.op("pe", lambda e, k=k, tb2=tb2: e.transpose(out=bk16(tb2, [8, 128])[:, k, :], in_=mg[:, k * 128:(k + 1) * 128], identity=ident[:]),
                         reads=["mg", "ident"], writes=[B(tb2)])
                S.op("act", lambda e, tb2=tb2: e.activation(out=mgT[:], in_=bk16(tb2, [8, 128]), func=AF.Copy), reads=[B(tb2)], writes=["mgT"])
                ob = tix % 2
                for c in range(2):
                    b = nb()
                    for k in range(8):
                        S.op("pe", lambda e, b=b, c=c, k=k: e.matmul(bk(b), lhsT=mgT[:, k, :], rhs=wo[:, k, c * 512:(c + 1) * 512], start=(k == 0), stop=(k == 7)),
                             reads=["mgT", "wdst"], writes=[B(b)])
                    S.op("dve", lambda e, b=b, c=c, xb=xb, ob=ob: e.tensor_tensor(out=x1t[ob][:, c * 512:(c + 1) * 512], in0=bk(b), in1=xtC[xb][:, c * 512:(c + 1) * 512], op=ALU.add),
                         reads=[B(b), "xt%d" % xb], writes=["x1t%d" % ob])
                dma(x1D[r0:r0 + 128, :], x1t[ob][:], reads=["x1t%d" % ob], writes=["x1D"])
                tix += 1
        S.barrier()

        wup = view(0, [8, 4096], BF16)
        wdn = view(64, [32, 1024], BF16)
        wpgt = view(128, [8, 1024], BF16)
        wplt = view(144, [2, 1024], BF16)
        load_scaled(wup, w_up[0], 8, 4096, 1, 148)
        S.barrier()
        load_scaled(wpgt, w_pg[0], 8, 1024, 2, 148)
        load_cast(wdn[:], w_down[0])
        load_cast(wplt[:], w_ple[0])
        S.barrier()
        E0 = 148
        x1 = [view(E0 + 4 * i, [D], F32) for i in range(2)]
        junkE = view(E0 + 8, [D], BF16)
        xbE = [view(E0 + 10 + 2 * i, [D], BF16) for i in range(2)]
        h2T = view(E0 + 14, [8, 256], BF16)
        rl = [view(E0 + 18 + i, [256], BF16) for i in range(2)]
        hid = [view(E0 + 20 + i, [256], BF16) for i in range(3)]
        x2 = [view(E0 + 23 + 4 * i, [D], F32) for i in range(2)]
        h3T = view(179, [8, 128], BF16)
        pt = view(181, [256], F32)
        pb = view(182, [256], BF16)
        pT = view(183, [2, 128], BF16)
        sgp = view(184, [D], F32)
        x3 = view(188, [D], F32)

        tix = 0
        for st_ in range(TOK // 256):
            rs = []
            for i in range(2):
                r0 = st_ * 256 + i * 128
                dma(x1[i][:], x1D[r0:r0 + 128, :], reads=["x1D"], writes=["x1_%d" % i])
                slot = 2 * (tix % 8)
                rstd = rms_stats(x1[i][:], "x1_%d" % i, slot, junkE)
                sq = small[:, slot + 1:slot + 2]
                S.op("dve", lambda e, sq=sq, rstd=rstd: e.tensor_tensor(out=sq, in0=rstd, in1=rstd, op=ALU.mult), reads=["sm%d" % slot], writes=["sq%d" % slot])
                rs.append((slot, sq))
                S.op("pool", lambda e, i=i: e.tensor_copy(out=xbE[i][:], in_=x1[i][:]), reads=["x1_%d" % i], writes=["xb16%d" % i])
                tb = 6 + (tix % 2)
                for k in range(8):
                    S.op("pe", lambda e, k=k, i=i, tb=tb: e.transpose(out=bk16(tb, [8, 128])[:, k, :], in_=xbE[i][:, k * 128:(k + 1) * 128], identity=ident[:]),
                         reads=["xb16%d" % i, "ident"], writes=[B(tb)])
                S.op("act", lambda e, i=i, tb=tb: e.activation(out=h2T[:, :, i * 128:(i + 1) * 128], in_=bk16(tb, [8, 128]), func=AF.Copy), reads=[B(tb)], writes=["h2T"])
                tix += 1

            def up(f):
                b = f % 2
                for k in range(8):
                    S.op("pe", lambda e, b=b, f=f, k=k: e.matmul(bk(b)[:, 0:256], lhsT=wup[:, k, f * 128:(f + 1) * 128], rhs=h2T[:, k, :], start=(k == 0), stop=(k == 7)),
                         reads=["wdst", "h2T"], writes=[B(b)])
                S.op("act", lambda e, b=b: e.activation(out=rl[b][:], in_=bk(b)[:, 0:256], func=AF.Relu), reads=[B(b)], writes=["rl%d" % b])
                hb = f % 3
                eng = "pool" if f % 2 == 0 else "dve"
                S.op(eng, lambda e, b=b, hb=hb: e.tensor_tensor(out=hid[hb][:], in0=rl[b][:], in1=rl[b][:], op=ALU.mult), reads=["rl%d" % b], writes=["hid%d" % hb])

            def down(f):
                hb = f % 3
                for i in range(2):
                    for c in range(2):
                        S.op("pe", lambda e, f=f, i=i, c=c, hb=hb: e.matmul(bk(2 + 2 * i + c), lhsT=hid[hb][:, i * 128:(i + 1) * 128], rhs=wdn[:, f, c * 512:(c + 1) * 512], start=(f == 0), stop=(f == 31)),
                             reads=["hid%d" % hb, "wdst"], writes=[B(2 + 2 * i + c)])

            up(0)
            for f in range(32):
                if f + 1 < 32:
                    up(f + 1)
                down(f)
            for i in range(2):
                slot, sq = rs[i]
                r0 = st_ * 256 + i * 128
                for c in range(2):
                    S.op("dve", lambda e, i=i, c=c, sq=sq: e.scalar_tensor_tensor(out=x2[i][:, c * 512:(c + 1) * 512], in0=bk(2 + 2 * i + c), scalar=sq, in1=x1[i][:, c * 512:(c + 1) * 512], op0=ALU.mult, op1=ALU.add),
                         reads=[B(2 + 2 * i + c), "sq%d" % slot, "x1_%d" % i], writes=["x2_%d" % i])
                slot3 = 16 + 2 * ((st_ * 2 + i) % 8)
                rstd3 = rms_stats(x2[i][:], "x2_%d" % i, slot3, junkE)
                S.op("pool", lambda e, i=i: e.tensor_copy(out=xbE[i][:], in_=x2[i][:]), reads=["x2_%d" % i], writes=["xb16%d" % i])
                tb = 6 + (i % 2)
                for k in range(8):
                    S.op("pe", lambda e, k=k, i=i, tb=tb: e.transpose(out=bk16(tb, [8, 128])[:, k, :], in_=xbE[i][:, k * 128:(k + 1) * 128], identity=ident[:]),
                         reads=["xb16%d" % i, "ident"], writes=[B(tb)])
                S.op("act", lambda e, tb=tb: e.activation(out=h3T[:], in_=bk16(tb, [8, 128]), func=AF.Copy), reads=[B(tb)], writes=["h3T"])
                dma(pt[:], p[r0:r0 + 128, :], writes=["pt"])
                S.op("pool", lambda e: e.tensor_copy(out=pb[:], in_=pt[:]), reads=["pt"], writes=["pb"])
                tb2 = 6 + ((i + 1) % 2)
                for k in range(2):
                    S.op("pe", lambda e, k=k, tb2=tb2: e.transpose(out=bk16(tb2, [8, 128])[:, k, :], in_=pb[:, k * 128:(k + 1) * 128], identity=ident[:]),
                         reads=["pb", "ident"], writes=[B(tb2)])
                S.op("act", lambda e, tb2=tb2: e.activation(out=pT[:], in_=bk16(tb2, [8, 128])[:, 0:2, :], func=AF.Copy), reads=[B(tb2)], writes=["pT"])
                for c in range(2):
                    b = c
                    for k in range(8):
                        S.op("pe", lambda e, b=b, c=c, k=k: e.matmul(bk(b), lhsT=h3T[:, k, :], rhs=wpgt[:, k, c * 512:(c + 1) * 512], start=(k == 0), stop=(k == 7)),
                             reads=["h3T", "wdst"], writes=[B(b)])
                    S.op("act", lambda e, b=b, c=c, rstd3=rstd3: e.activation(out=sgp[:, c * 512:(c + 1) * 512], in_=bk(b), func=AF.Sigmoid, scale=rstd3),
                         reads=[B(b), "sm%d" % slot3], writes=["sgp%d" % c])
                for c in range(2):
                    b = c
                    for k in range(2):
                        S.op("pe", lambda e, b=b, c=c, k=k: e.matmul(bk(b), lhsT=pT[:, k, :], rhs=wplt[:, k, c * 512:(c + 1) * 512], start=(k == 0), stop=(k == 1)),
                             reads=["pT", "wdst"], writes=[B(b)])
                    S.op("dve", lambda e, b=b, c=c: e.tensor_tensor(out=x3[:, c * 512:(c + 1) * 512], in0=bk(b), in1=sgp[:, c * 512:(c + 1) * 512], op=ALU.mult),
                         reads=[B(b), "sgp%d" % c], writes=["x3_%d" % c])
                    S.op("pool", lambda e, i=i, c=c: e.tensor_tensor(out=x3[:, c * 512:(c + 1) * 512], in0=x3[:, c * 512:(c + 1) * 512], in1=x2[i][:, c * 512:(c + 1) * 512], op=ALU.add),
                         reads=["x3_%d" % c, "x2_%d" % i], writes=["x3_%d" % c])
                slot4 = 32 + 2 * ((st_ * 2 + i) % 8)
                ssv = small[:, slot4:slot4 + 1]
                S.op("act", lambda e, ssv=ssv: e.activation(out=junkE[:], in_=x3[:], func=AF.Square, accum_out=ssv), reads=["x3_0", "x3_1"], writes=["junk", "sm%d" % slot4])
                S.op("dve", lambda e, ssv=ssv: e.tensor_scalar(out=ssv, in0=ssv, scalar1=1.0 / D, scalar2=EPS, op0=ALU.mult, op1=ALU.add), reads=["sm%d" % slot4], writes=["sm%d" % slot4])
                S.op("pool", lambda e, ssv=ssv: e.tensor_tensor(out=ssv, in0=ssv, in1=mhalf[:], op=ALU.pow), reads=["sm%d" % slot4, "mhalf"], writes=["sm%d" % slot4])
                S.op("dve", lambda e, ssv=ssv: e.scalar_tensor_tensor(out=x3[:], in0=x3[:], scalar=ssv, in1=gfin[:], op0=ALU.mult, op1=ALU.mult),
                     reads=["x3_0", "x3_1", "sm%d" % slot4, "gfin"], writes=["x3_0", "x3_1"])
                dma(out[r0:r0 + 128, :], x3[:], reads=["x3_0", "x3_1"], writes=["outD"])
        S.final_wait("sp")
        S.emit(block)
    return nc


_CONST = {}


def _consts():
    if _CONST:
        return _CONST
    bf = ml_dtypes.bfloat16
    c = np.arange(128)
    ang = 2.0 * np.pi * ((c[:, None] * c[None, :]) % 128) / 128.0
    dftc = np.concatenate([np.cos(ang), np.sin(ang)], axis=1) / 512.0
    s = np.arange(SEQ, dtype=np.int64)
    ang2 = 2.0 * np.pi * ((s[:, None] * s[None, :]) % SEQ).astype(np.float64) / SEQ
    tbl = np.stack([np.cos(ang2), -np.sin(ang2)], axis=0)
    mask = (np.arange(128)[:, None] // 16 == np.arange(8)[None, :]).astype(np.float32)
    _CONST.update(
        c_ident=np.eye(128).astype(bf),
        c_ident32=np.eye(128, dtype=np.float32),
        c_dftc=dftc.astype(bf),
        c_tbl=tbl.astype(bf),
        c_mask=mask,
        c_tidx=np.tile(np.arange(SEQ, dtype=np.float32)[None, :], (128, 1)),
    )
    return _CONST


_NC = {}


def kernel(**inputs):
    inp = {k: np.ascontiguousarray(np.asarray(v)) for k, v in inputs.items()}
    consts = _consts()
    if "nc" not in _NC:
        _NC["nc"] = build()
    nc = _NC["nc"]
    x = inp["x"].reshape(NCORES, TOK, D)
    p = inp["p"].reshape(NCORES, TOK, 256)
    shared = {k: v for k, v in inp.items() if k not in ("x", "p")}
    in_maps = []
    for c in range(NCORES):
        m = dict(shared)
        m.update(consts)
        m["x"] = x[c]
        m["p"] = p[c]
        in_maps.append(m)
    res = run_bass_kernel_spmd(nc, in_maps, core_ids=list(range(NCORES)))
    outs = [np.asarray(r["out"]) for r in res.results]
    return np.stack(outs, 0).reshape(16, SEQ, D).astype(np.float32)
```
